# Optimizing a Trainium2 kernel written in Bass

```python
import math
import jax, jax.numpy as jnp
from jax import lax
import numpy as np

D_MODEL = 2048
BATCH = 2
SEQ = 4096
DEPTH = 2
DEC_BATCH = 8
DEC_SEQ = 1
PAST_LEN = 16384
PAGE_SIZE = 128

A_GROUPS = ((128, 1), (512, 4), (2048, 16))
A_HEADS = 8
A_HEAD_DIM = 128
A_WIDTH = A_HEADS * A_HEAD_DIM
A_BLOCK = 128
ROPE_THETA = 10000.0
B_WINDOWS = (2, 4, 8, 16)
B_WIDTH = D_MODEL // 2
B_GROUP = B_WIDTH // len(B_WINDOWS)
B_STATE = max(B_WINDOWS) - 1
C_CHUNK = 128
C_WIDTH = D_MODEL // 2
C_GROUPS = 4
C_GROUP = C_WIDTH // C_GROUPS
N_BRANCH = 3
FF_HIDDEN = ((8 * D_MODEL // 3 + 255) // 256) * 256
N_IN = len(A_GROUPS) * 3 * A_WIDTH + B_WIDTH + 2 * C_WIDTH + N_BRANCH * D_MODEL
EPS = 1e-6

kernel_name = 'hybrid_dilated_pool_gmlp_decoder_step'


def rms_norm(x, g):
    xf = x.astype(jnp.float32)
    y = xf * lax.rsqrt(jnp.mean(xf * xf, axis=-1, keepdims=True) + EPS)
    return (y * g.astype(jnp.float32)).astype(x.dtype)


def rope(x, pos):
    half = x.shape[-1] // 2
    inv_freq = ROPE_THETA ** (-jnp.arange(half, dtype=jnp.float32) / half)
    ang = pos.astype(jnp.float32)[:, None] * inv_freq[None, :]
    cos = jnp.cos(ang)[None, :, None, :]
    sin = jnp.sin(ang)[None, :, None, :]
    xf = x.astype(jnp.float32)
    x1, x2 = xf[..., :half], xf[..., half:]
    return jnp.concatenate([x1 * cos - x2 * sin, x2 * cos + x1 * sin], axis=-1).astype(x.dtype)


def band_attention(q, k, v, reach):
    N, n, H, D = q.shape
    nb = n // A_BLOCK
    qb = q.reshape(N, nb, A_BLOCK, H, D)
    kb = k.reshape(N, nb, A_BLOCK, H, D)
    vb = v.reshape(N, nb, A_BLOCK, H, D)

    def with_prev(t):
        prev = jnp.concatenate([jnp.zeros_like(t[:, :1]), t[:, :-1]], axis=1)
        return jnp.concatenate([prev, t], axis=2)

    kx, vx = with_prev(kb), with_prev(vb)
    s = jnp.einsum('nbqhd,nbkhd->nbhqk', qb, kx).astype(jnp.float32)
    qi = jnp.arange(A_BLOCK)[:, None] + A_BLOCK
    kj = jnp.arange(2 * A_BLOCK)[None, :]
    dist = qi - kj
    band = (dist >= 0) & (dist <= reach)
    has_prev = (jnp.arange(nb)[:, None, None] > 0) | (kj[None] >= A_BLOCK)
    mask = band[None] & has_prev
    s = jnp.where(mask[None, :, None], s, -jnp.inf)
    lse = jax.nn.logsumexp(s, axis=-1)
    p = jnp.exp(s - lse[..., None]).astype(vx.dtype)
    o = jnp.einsum('nbhqk,nbkhd->nbqhd', p, vx)
    return o.reshape(N, n, H, D), jnp.transpose(lse, (0, 1, 3, 2)).reshape(N, n, H)


def dilated_attention_prompt(q, k, v, dil, reach):
    B, S, H, D = q.shape
    n = S // dil
    n_pad = -(-n // A_BLOCK) * A_BLOCK

    def to_residue(t):
        t = t.reshape(B, n, dil, H, D).transpose(0, 2, 1, 3, 4).reshape(B * dil, n, H, D)
        return jnp.pad(t, ((0, 0), (0, n_pad - n), (0, 0), (0, 0)))

    o, lse = band_attention(to_residue(q), to_residue(k), to_residue(v), reach)
    o = o[:, :n].reshape(B, dil, n, H, D).transpose(0, 2, 1, 3, 4).reshape(B, S, H, D)
    lse = lse[:, :n].reshape(B, dil, n, H).transpose(0, 2, 1, 3).reshape(B, S, H)
    return o, lse


def dilated_attention_sample(q, k_ext, v_ext, dil, reach, buf_len):
    T = q.shape[1]
    idx = buf_len + jnp.arange(T)[:, None] - dil * jnp.arange(reach + 1)[None, :]
    valid = idx >= 0
    idx = jnp.maximum(idx, 0)
    kg = k_ext[:, idx]
    vg = v_ext[:, idx]
    s = jnp.einsum('bthd,btjhd->bhtj', q, kg).astype(jnp.float32)
    s = jnp.where(valid[None, None], s, -jnp.inf)
    lse = jax.nn.logsumexp(s, axis=-1)
    p = jnp.exp(s - lse[..., None]).astype(vg.dtype)
    o = jnp.einsum('bhtj,btjhd->bthd', p, vg)
    return o, jnp.transpose(lse, (0, 2, 1))


def merge_dilations(outs, lses):
    w = jax.nn.softmax(jnp.stack(lses, axis=0), axis=0)
    o = jnp.stack(outs, axis=0).astype(jnp.float32)
    return jnp.sum(o * w[..., None], axis=0).astype(outs[0].dtype)


def pool_mixer(ext, pos, pool_w, pool_scale):
    B, L, _ = ext.shape
    T = pos.shape[0]
    xf = ext.astype(jnp.float32)
    cs = jnp.cumsum(xf, axis=1)
    cs = jnp.concatenate([jnp.zeros_like(cs[:, :1]), cs], axis=1)
    end = cs[:, L - T + 1:]
    x_new = xf[:, L - T:]
    diffs = []
    for gi, w in enumerate(B_WINDOWS):
        sl = slice(gi * B_GROUP, (gi + 1) * B_GROUP)
        start = cs[:, L - T + 1 - w: L + 1 - w, sl]
        cnt = jnp.minimum(pos + 1, w).astype(jnp.float32)[None, :, None]
        diffs.append((end[..., sl] - start) / cnt - x_new[..., sl])
    d = jnp.stack(diffs, axis=2)
    y = jnp.einsum('btgc,gce->btge', d, pool_w.astype(jnp.float32)).reshape(B, T, B_WIDTH)
    return (y * pool_scale.astype(jnp.float32)).astype(ext.dtype)


def spatial_gate(u, v, ws, bias):
    B, T, _ = v.shape
    n_pad = -(-T // C_CHUNK) * C_CHUNK
    vp = jnp.pad(v, ((0, 0), (0, n_pad - T), (0, 0))).reshape(B, n_pad // C_CHUNK, C_CHUNK, C_GROUPS, C_GROUP)
    f = jnp.einsum('gij,bcjgd->bcigd', jnp.tril(ws), vp) + bias.T[None, None, :, :, None]
    return u * f.reshape(B, n_pad, C_WIDTH)[:, :T].astype(u.dtype)


def swiglu(h, w_gu, w_down):
    gu = h @ w_gu
    return (jax.nn.silu(gu[..., :FF_HIDDEN]) * gu[..., FF_HIDDEN:]) @ w_down


def layer(x, pos, kv_bufs, pool_buf, norm1_g, w_in, wo_a, pool_w, pool_scale, wo_b,
          c_norm_g, c_ws, c_bias, wo_c, gate_bias, w_out, norm2_g, w_gu, w_down):
    B, T, _ = x.shape
    h = rms_norm(x, norm1_g)
    z = h @ w_in
    o1 = len(A_GROUPS) * 3 * A_WIDTH
    o2 = o1 + B_WIDTH
    o3 = o2 + 2 * C_WIDTH
    za = z[..., :o1].reshape(B, T, len(A_GROUPS), 3, A_HEADS, A_HEAD_DIM)
    xb = z[..., o1:o2]
    zc = jax.nn.gelu(z[..., o2:o3])
    gates = jax.nn.sigmoid(z[..., o3:].reshape(B, T, N_BRANCH, D_MODEL) + gate_bias)

    outs, lses, new_kv = [], [], []
    for gi, (win, dil) in enumerate(A_GROUPS):
        reach = win // dil
        q = rope(za[:, :, gi, 0], pos) * (A_HEAD_DIM ** -0.5)
        k = rope(za[:, :, gi, 1], pos)
        v = za[:, :, gi, 2]
        kv_new = jnp.stack([k, v], axis=2)
        if kv_bufs is None:
            o, lse = dilated_attention_prompt(q, k, v, dil, reach)
            keep = min(win, T)
            new_kv.append(kv_new[:, T - keep:])
        else:
            buf = kv_bufs[gi]
            L = buf.shape[1]
            kv_ext = jnp.concatenate([buf.astype(kv_new.dtype), kv_new], axis=1)
            o, lse = dilated_attention_sample(q, kv_ext[:, :, 0], kv_ext[:, :, 1], dil, reach, L)
            new_kv.append(kv_ext[:, T:])
        outs.append(o)
        lses.append(lse)
    ya = merge_dilations(outs, lses).reshape(B, T, A_WIDTH) @ wo_a

    if pool_buf is None:
        ext = jnp.pad(xb, ((0, 0), (B_STATE, 0), (0, 0)))
    else:
        ext = jnp.concatenate([pool_buf.astype(xb.dtype), xb], axis=1)
    yb = pool_mixer(ext, pos, pool_w, pool_scale) @ wo_b
    new_pool = ext[:, ext.shape[1] - B_STATE:]

    u = zc[..., :C_WIDTH]
    vc = rms_norm(zc[..., C_WIDTH:], c_norm_g)
    yc = spatial_gate(u, vc, c_ws, c_bias) @ wo_c

    merged = gates[:, :, 0] * ya + gates[:, :, 1] * yb + gates[:, :, 2] * yc
    x = x + merged @ w_out
    x = x + swiglu(rms_norm(x, norm2_g), w_gu, w_down)
    return x, new_kv, new_pool, vc


def setup_inputs(seed: int = 0) -> dict:
    key = jax.random.key(seed)
    ks = jax.random.split(key, 24)
    f32 = jnp.float32

    def nrm(k, shape, scale):
        return jax.random.normal(k, shape, f32) * scale

    def gain(k, shape):
        return 1.0 + 0.02 * jax.random.normal(k, shape, f32)

    kv_shape = lambda w: (DEPTH, DEC_BATCH, min(w, PAST_LEN), 2, A_HEADS, A_HEAD_DIM)
    return {
        'x_prompt': nrm(ks[0], (BATCH, SEQ, D_MODEL), 1.0),
        'x_sample': nrm(ks[1], (DEC_BATCH, DEC_SEQ, D_MODEL), 1.0),
        'cache_a1_kv': nrm(ks[2], kv_shape(A_GROUPS[0][0]), 1.0),
        'cache_a2_kv': nrm(ks[3], kv_shape(A_GROUPS[1][0]), 1.0),
        'cache_a3_kv': nrm(ks[4], kv_shape(A_GROUPS[2][0]), 1.0),
        'state_b_pool': nrm(ks[5], (DEPTH, DEC_BATCH, B_STATE, B_WIDTH), 1.0),
        'norm1_g': gain(ks[6], (DEPTH, D_MODEL)),
        'w_in': nrm(ks[7], (DEPTH, D_MODEL, N_IN), D_MODEL ** -0.5),
        'wo_a': nrm(ks[8], (DEPTH, A_WIDTH, D_MODEL), A_WIDTH ** -0.5),
        'pool_w': nrm(ks[9], (DEPTH, len(B_WINDOWS), B_GROUP, B_GROUP), B_GROUP ** -0.5),
        'pool_scale': gain(ks[10], (DEPTH, B_WIDTH)),
        'wo_b': nrm(ks[11], (DEPTH, B_WIDTH, D_MODEL), B_WIDTH ** -0.5),
        'c_norm_g': gain(ks[12], (DEPTH, C_WIDTH)),
        'c_ws': nrm(ks[13], (DEPTH, C_GROUPS, C_CHUNK, C_CHUNK), C_CHUNK ** -0.5),
        'c_bias': gain(ks[14], (DEPTH, C_GROUPS, C_CHUNK)),
        'wo_c': nrm(ks[15], (DEPTH, C_WIDTH, D_MODEL), C_WIDTH ** -0.5),
        'gate_bias': nrm(ks[16], (DEPTH, N_BRANCH, D_MODEL), 0.01),
        'w_out': nrm(ks[17], (DEPTH, D_MODEL, D_MODEL), D_MODEL ** -0.5),
        'norm2_g': gain(ks[18], (DEPTH, D_MODEL)),
        'w_gu': nrm(ks[19], (DEPTH, D_MODEL, 2 * FF_HIDDEN), D_MODEL ** -0.5),
        'w_down': nrm(ks[20], (DEPTH, FF_HIDDEN, D_MODEL), FF_HIDDEN ** -0.5),
        'final_norm_g': gain(ks[21], (D_MODEL,)),
    }


def reference(x_prompt, x_sample, cache_a1_kv, cache_a2_kv, cache_a3_kv, state_b_pool,
              norm1_g, w_in, wo_a, pool_w, pool_scale, wo_b, c_norm_g, c_ws, c_bias, wo_c,
              gate_bias, w_out, norm2_g, w_gu, w_down, final_norm_g):
    pos_p = jnp.arange(x_prompt.shape[1], dtype=jnp.int32)
    pos_s = PAST_LEN + jnp.arange(x_sample.shape[1], dtype=jnp.int32)
    hp, hs = x_prompt, x_sample
    kv_p = [[], [], []]
    kv_s = [[], [], []]
    pool_p, pool_s, cv_s = [], [], []
    for l in range(DEPTH):
        lw = (norm1_g[l], w_in[l], wo_a[l], pool_w[l], pool_scale[l], wo_b[l], c_norm_g[l],
              c_ws[l], c_bias[l], wo_c[l], gate_bias[l], w_out[l], norm2_g[l], w_gu[l], w_down[l])
        hp, nkv_p, npool_p, _ = layer(hp, pos_p, None, None, *lw)
        hs, nkv_s, npool_s, ncv_s = layer(
            hs, pos_s, (cache_a1_kv[l], cache_a2_kv[l], cache_a3_kv[l]), state_b_pool[l], *lw)
        for gi in range(len(A_GROUPS)):
            kv_p[gi].append(nkv_p[gi])
            kv_s[gi].append(nkv_s[gi])
        pool_p.append(npool_p)
        pool_s.append(npool_s)
        cv_s.append(ncv_s)
    y_prompt = rms_norm(hp, final_norm_g)
    y_sample = rms_norm(hs, final_norm_g)
    return (y_prompt, y_sample,
            jnp.stack(kv_p[0]), jnp.stack(kv_p[1]), jnp.stack(kv_p[2]),
            jnp.stack(kv_s[0]), jnp.stack(kv_s[1]), jnp.stack(kv_s[2]),
            jnp.stack(pool_p), jnp.stack(pool_s), jnp.stack(cv_s))
```

```python
import contextlib
import numpy as np
import ml_dtypes
import concourse.bass as bass
import concourse.mybir as mybir
from concourse.bass_utils import run_bass_kernel_spmd

F32 = mybir.dt.float32
BF16 = mybir.dt.bfloat16
AF = mybir.ActivationFunctionType
ALU = mybir.AluOpType

D = 2048
KC = 16
T = 512
SC = 512
RW = 520
SEQ = 4096
NTILES = SEQ // T
PAST = 16384
NIN = 18432
FFH = 5632
O1 = 9216
O2 = O1 + 1024
O3 = O2 + 2048
GRP = ((1, 128, 4, 128), (4, 512, 4, 128), (16, 2048, 16, 32))
WINS = (2, 4, 8, 16)
EPS = 1e-6
QSCALE = 128 ** -0.5


class Buf:
    __slots__ = ("name", "w", "r", "lock")

    def __init__(self, name, lock=None):
        self.name = name
        self.w = None
        self.r = []
        self.lock = lock


class Emit:
    RING = 12

    def __init__(self, nc, es):
        self.nc = nc
        self.es = es
        self.E = {"pe": nc.tensor, "act": nc.scalar, "dve": nc.vector, "pool": nc.gpsimd, "sp": nc.sync}
        self.semh = {}
        self.cnt = {}
        self.epoch = 0
        self.waited = {e: {} for e in self.E}
        self.pend = {e: [] for e in self.E}
        self.ring_i = {"sp": 0, "pool": 0}
        self.n_ins = 0

    def sem(self, key):
        if key not in self.semh:
            self.semh[key] = self.es.enter_context(self.nc.semaphore("s_" + "_".join(str(k) for k in key)))
            self.cnt[key] = 0
        return self.semh[key]

    def _wait(self, eng, toks):
        best = {}
        for t in toks:
            if t is None:
                continue
            key, val = t
            if eng == "pe" and key[0] == "pe":
                continue
            if self.waited[eng].get(key, 0) >= val:
                continue
            if best.get(key, 0) < val:
                best[key] = val
        for key, val in best.items():
            self.E[eng].wait_ge(self.semh[key], val)
            self.waited[eng][key] = val
            self.n_ins += 1

    @staticmethod
    def _deps(R, W):
        toks = []
        for b in R:
            toks.extend(b.w if isinstance(b.w, list) else [b.w])
        for b in W:
            toks.extend(b.w if isinstance(b.w, list) else [b.w])
            toks.extend(b.r)
        return toks

    def _record(self, tok, R, W):
        for b in R:
            b.r = [t for t in b.r if t[0] != tok[0]] + [tok]
        for b in W:
            b.w = tok
            b.r = []

    limit = None
    n_op = 0

    def _skip(self):
        self.n_op += 1
        return self.limit is not None and self.n_op > self.limit

    def op(self, eng, fn, R=(), W=(), sig=True):
        if self._skip():
            return None
        locks = []
        if eng != "pe":
            for b in R:
                if b.lock is not None and b.lock not in locks:
                    locks.append(b.lock)
        lk = [lo.w for lo in locks if lo.w is not None and lo.w[0][0] != eng]
        self._wait(eng, self._deps(R, W) + lk)
        ins = fn(self.E[eng])
        self.n_ins += 1
        if not sig:
            self.pend[eng].append((tuple(R), tuple(W)))
            return None
        key = (eng, self.epoch)
        self.sem(key)
        ins.then_inc(self.semh[key], 1)
        self.cnt[key] += 1
        tok = (key, self.cnt[key])
        for (pr, pw) in self.pend[eng]:
            self._record(tok, pr, pw)
        self.pend[eng] = []
        self._record(tok, R, W)
        for lo in locks:
            lo.w = tok
        return tok

    def dma(self, q, out, in_, R=(), W=(), **kw):
        if self._skip():
            return None
        i = self.ring_i[q]
        self.ring_i[q] = (i + 1) % self.RING
        key = ("d" + q, i)
        self.sem(key)
        prev = (key, self.cnt[key]) if self.cnt[key] else None
        self._wait(q, self._deps(R, W) + [prev])
        self.E[q].dma_start(out=out, in_=in_, **kw).then_inc(self.semh[key], 16)
        self.n_ins += 1
        self.cnt[key] += 16
        tok = (key, self.cnt[key])
        self._record(tok, R, W)
        return tok

    def drain(self, eng, bufs):
        self._wait(eng, self._deps((), bufs))


def _consts():
    half = 64
    inv = (10000.0 ** (-np.arange(half, dtype=np.float32) / half)).astype(np.float32)
    cs = np.zeros((NTILES, 2, 128, RW), np.float32)
    for ti in range(NTILES):
        pos = np.zeros(RW, np.float32)
        pos[:T] = np.arange(ti * T, (ti + 1) * T, dtype=np.float32)
        pos[SC] = float(PAST)
        ang = pos[None, :] * inv[:, None]
        c = np.cos(ang).astype(np.float32)
        s = np.sin(ang).astype(np.float32)
        cs[ti, 0, :64] = c
        cs[ti, 0, 64:] = c
        cs[ti, 1, :64] = -s
        cs[ti, 1, 64:] = s
    kk = np.arange(128)[:, None]
    qq = np.arange(128)[None, :]
    cb = np.zeros((5, 128, 128), np.float32)
    cb[0] = np.eye(128)
    cb[1] = (kk == (qq + 64) % 128)
    cb[2] = 1.0
    cb[3] = (qq <= kk)
    cb[4] = (qq >= kk)
    rc = np.ones((128, 4, 16), np.float32)
    for gi, w in enumerate(WINS):
        rc[:, gi, :15] = 1.0 / np.minimum(np.arange(15) + 1, w)
    sel = np.zeros((15, 4), np.float32)
    for gi, w in enumerate(WINS):
        sel[16 - w:, gi] = 1.0
    return cs, cb.astype(ml_dtypes.bfloat16), rc, sel


class _Stop(Exception):
    pass


def build(NT=NTILES, NL=2, sample=True, stop=None):
    nc = bass.Bass("TRN2", target_bir_lowering=False)

    def chk(name):
        if stop == name:
            raise _Stop()

    def din(name, shape, dt=F32):
        return nc.dram_tensor(name, list(shape), dt, kind="ExternalInput").ap()

    def dout(name, shape, dt=F32):
        return nc.dram_tensor(name, list(shape), dt, kind="ExternalOutput").ap()

    xT_in = din("xT_in", [D, SEQ])
    xs_in = din("xs_in", [D, 1])
    cache = [din("c1", [2, 128, 2, 8, 128]), din("c2", [2, 512, 2, 8, 128]), din("c3", [2, 2048, 2, 8, 128])]
    pstate = din("pstate", [2, 15, 1024])
    norm1_g = din("norm1_g", [2, D])
    w_in = din("w_in", [2, D, NIN])
    wo = [din("wo_a", [2, 1024, D]), din("wo_b", [2, 1024, D]), din("wo_c", [2, 1024, D])]
    pool_w = din("pool_w", [2, 4, 256, 256])
    pool_scale = din("pool_scale", [2, 1024])
    c_norm_g = din("c_norm_g", [2, 1024])
    c_ws = din("c_ws", [2, 4, 128, 128])
    c_bias = din("c_bias", [2, 4, 128])
    gate_bias = din("gate_bias", [2, 3, D])
    w_out = din("w_out", [2, D, D])
    norm2_g = din("norm2_g", [2, D])
    w_gu = din("w_gu", [2, D, 2 * FFH])
    w_down = din("w_down", [2, FFH, D])
    final_g = din("final_g", [1, D])
    cs_in = din("cs_in", [NTILES, 2, 128, RW])
    cb_in = din("cb_in", [5, 128, 128], BF16)
    rc_in = din("rc_in", [128, 4, 16])
    sel_in = din("sel_in", [15, 4])

    yT = dout("yT", [D, SEQ])
    ys = dout("ys", [D, 1])
    kvp = [dout("kvp1", [2, 2, 8, 128, 128]), dout("kvp2", [2, 2, 8, 128, 512]), dout("kvp3", [2, 2, 8, 128, 2048])]
    kvs = [dout("kvs1", [2, 128, 2, 8, 128]), dout("kvs2", [2, 512, 2, 8, 128]), dout("kvs3", [2, 2048, 2, 8, 128])]
    poolp = dout("poolp", [2, 1024, 15])
    pools = dout("pools", [2, 15, 1024])
    cvs = dout("cvs", [2, 1024])
    hist = nc.dram_tensor("hist", [2, 3, 2, 8, 128, SEQ], BF16, kind="Internal").ap()

    es = contextlib.ExitStack()
    with es:
        em = Emit(nc, es)
        import os as _os
        if _os.environ.get("EMIT_LIMIT"):
            em.limit = int(_os.environ["EMIT_LIMIT"])

        def sb(name, shape, dt=F32):
            return es.enter_context(nc.sbuf_tensor(name, list(shape), dt))

        def ps(name, shape, dt=F32):
            return es.enter_context(nc.psum_tensor(name, list(shape), dt))

        xT = sb("xT", [128, KC, RW]); xTb = [Buf(f"xT{c}") for c in range(KC)]
        hT = sb("hT", [128, KC, RW], BF16); hTb = Buf("hT")
        NWB = 3
        wb = [sb(f"wb{i}", [128, 8192], BF16) for i in range(NWB)]
        wbb = [Buf(f"wb{i}") for i in range(NWB)]
        wst = {"i": 0}
        cst = sb("cst", [128, 2, RW]); cstb = Buf("cst")
        cbt = sb("cbt", [128, 5, 128], BF16); cbb = Buf("cb")
        ident, rswap, ones, mprev, mcur = (cbt[:, i, :] for i in range(5))
        g1v = sb("g1v", [128, 2, KC]); g2v = sb("g2v", [128, 2, KC]); gfv = sb("gfv", [128, 1, KC])
        gbv = sb("gbv", [128, 2, 48]); psv = sb("psv", [128, 2, 8])
        smallb = Buf("small")
        rct = sb("rct", [128, 4, 16]); selt = sb("selt", [15, 4])
        sq = [sb(f"sq{i}", [128, RW], BF16) for i in range(2)]; sqb = [Buf("sq0"), Buf("sq1")]
        rstd = sb("rstd", [128, RW]); rstdb = Buf("rstd")
        ARENA = 73 * 1024
        arena = sb("arena", [128, ARENA // 2], BF16)
        cur_off = {"o": 0}

        def AR(shape, dt=F32, at=None):
            if at is not None:
                cur_off["o"] = at
            off = cur_off["o"]
            esz = 4 if dt == F32 else 2
            n = int(np.prod(shape[1:]))
            nb_ = (n * esz + 31) // 32 * 32
            assert off + nb_ <= ARENA, (off, nb_, shape)
            v = arena[:, off // 2:off // 2 + n * esz // 2]
            if dt == F32:
                v = v.bitcast(F32)
            if len(shape) == 3:
                v = v.rearrange("p (a b) -> p a b", a=shape[1])
            elif len(shape) == 4:
                v = v.rearrange("p (a b c) -> p a b c", a=shape[1], b=shape[2])
            if shape[0] < 128:
                v = v[0:shape[0]]
            cur_off["o"] = off + nb_
            return v

        oT = AR([128, 8, RW], BF16, at=0); oTb = [Buf(f"oT{c}") for c in range(8)]
        P1 = cur_off["o"]
        QT = AR([128, 3, RW], BF16); QTb = [Buf(f"QT{g}") for g in range(3)]
        KTt = [AR([128, GRP[g][1] + RW], BF16) for g in range(3)]
        VTt = [AR([128, GRP[g][1] + RW], BF16) for g in range(3)]
        KTh = [Buf(f"KTh{g}") for g in range(3)]; KTc = [Buf(f"KTc{g}") for g in range(3)]
        VTh = [Buf(f"VTh{g}") for g in range(3)]; VTc = [Buf(f"VTc{g}") for g in range(3)]
        NVB = (5, 8, 32)
        Vtok = [AR([128, NVB[g], 128], BF16) for g in range(3)]
        Vtokb = [Buf(f"Vtok{g}") for g in range(3)]
        zb = AR([128, RW], BF16); zbb = Buf("zb")
        PT = [AR([128, 512], BF16) for i in range(2)]; PTb = [Buf("PT0"), Buf("PT1")]
        rD = AR([128, RW]); rDb = Buf("rD")
        Kc = AR([128, 3, 128], BF16); Kcb = Buf("Kc")
        Vc = AR([128, 3, 128], BF16); Vcb = Buf("Vc")
        KcT = AR([128, 3, 128], BF16); KcTb = Buf("KcT")
        vsr = AR([1, 3, 128], BF16); vsrb = Buf("vsr")
        pS = AR([128, 16], BF16); pSb = Buf("pS")
        G_ATT = QTb + KTh + KTc + VTh + VTc + Vtokb + [zbb, rDb, Kcb, Vcb, KcTb, vsrb, pSb] + PTb
        gateT = AR([128, KC, RW], BF16, at=P1); gateb = [Buf(f"gate{c}") for c in range(KC)]
        mrg = AR([128, KC, RW], BF16); mrgb = [Buf(f"mrg{c}") for c in range(KC)]
        P2 = cur_off["o"]
        xbT = AR([128, 8, 15 + RW + 1]); xbb = [Buf(f"xb{c}") for c in range(8)]
        pA = AR([128, 15 + RW + 1]); pB = AR([128, 15 + RW + 1]); pAb = Buf("pA"); pBb = Buf("pB")
        dT = AR([128, 8, RW], BF16); dTb = [Buf(f"dT{c}") for c in range(8)]
        G_B = xbb + [pAb, pBb] + dTb
        uT = AR([128, 8, RW], BF16, at=P2); uTb = [Buf(f"uT{c}") for c in range(8)]
        gv = AR([128, 1024]); gvb = Buf("gv")
        gsq = AR([128, 1024]); gsqb = Buf("gsq")
        vcs = gsq[0:1, :]; vcsb = gsqb
        vcb = AR([128, 5, 1024], BF16); vcbb = [Buf(f"vcb{i}") for i in range(5)]
        gnr = AR([128, 1024]); gnrb = Buf("gnr")
        G_C = uTb + [gvb, gsqb, gnrb] + vcbb
        G_MRG = gateb + mrgb + G_B + G_C
        actT = AR([128, 44, RW], BF16, at=0); actb = [Buf(f"act{c}") for c in range(44)]
        sg = [AR([128, RW]) for i in range(2)]; sgb = [Buf("sg0"), Buf("sg1")]
        yt = [AR([128, RW]) for i in range(2)]; ytb = [Buf("yt0"), Buf("yt1")]
        G_FFN = actb + sgb + ytb
        wsb = AR([128, 8, 128], BF16, at=0)
        t1 = [sb(f"t1_{i}", [128, RW]) for i in range(2)]; t1b = [Buf("t1_0"), Buf("t1_1")]
        t2 = sb("t2", [128, RW]); t2b = Buf("t2")
        xbH = sb("xbH", [128, 2, 8, 15]); xbHb = [Buf("xbH0"), Buf("xbH1")]
        pss = sb("pss", [128, 2, 8, 4]); pssb = Buf("pss")
        ssq = sb("ssq", [128, 4]); ssqb = Buf("ssq")
        trilT = sb("trilT", [128, 2, 4, 128], BF16); biasr = sb("biasr", [128, 2, 4, 128])
        w00 = sb("w00", [1, 8], BF16); w00f = sb("w00f", [1, 8])
        cprep = Buf("cprep")

        def toks(bufs):
            return [t for b in bufs for t in ((b.w if isinstance(b.w, list) else [b.w]) + b.r) if t]

        def compress(tl):
            best = {}
            for (k_, v_) in tl:
                if best.get(k_, 0) < v_:
                    best[k_] = v_
            return list(best.items())

        def barrier(new, old):
            tk = compress(toks(old))
            for b in new:
                b.r = compress(b.r + tk)

        def PB(name, lockname=None):
            return Buf(name, lock=Buf("lk_" + (lockname or name)))
        PJ = [ps(f"PJ{i}", [128, 512]) for i in range(2)]; PJb = [PB("PJ0"), PB("PJ1")]
        SBK = ps("SBK", [128, 512]); NSL = 8
        _sbk_lock = Buf("lk_SBK")
        _sbk_one = Buf("SBK", lock=_sbk_lock)
        SBKb = [_sbk_one for i in range(NSL)]
        AN = ps("AN", [128, 512]); AD = ps("AD", [128, 512]); ANb = PB("AN"); ADb = PB("AD")
        AS = [ps(f"AS{i}", [128, 512]) for i in range(2)]; ASb = [PB("AS0"), PB("AS1")]
        AT = ps("AT", [128, 1024], BF16); ATb = PB("AT")
        st = {"pj": 0, "slot": 0, "sq": 0, "t1": 0, "sg": 0, "yt": 0}

        def rot(key, n):
            v = st[key]
            st[key] = (v + 1) % n
            return v

        em.dma("sp", cbt[:], cb_in.rearrange("k p n -> p k n"), W=[cbb])
        with nc.allow_non_contiguous_dma(reason="small per-feature vectors"):
            em.dma("sp", g1v[:], norm1_g.rearrange("l (c p) -> p l c", p=128), W=[smallb])
            em.dma("sp", g2v[:], norm2_g.rearrange("l (c p) -> p l c", p=128), W=[smallb])
            em.dma("sp", gfv[:], final_g.rearrange("l (c p) -> p l c", p=128), W=[smallb])
            em.dma("sp", gbv[:], gate_bias.rearrange("l b (c p) -> p l (b c)", p=128), W=[smallb])
            em.dma("sp", psv[:], pool_scale.rearrange("l (c p) -> p l c", p=128), W=[smallb])
        em.dma("sp", rct[:], rc_in, W=[smallb])
        em.dma("sp", selt[:], sel_in, W=[smallb])
        em.dma("sp", biasr[:].rearrange("p l g i -> p (l g i)"),
               c_bias.rearrange("l g i -> (l g i)").rearrange("(o n) -> o n", o=1).broadcast_to([128, 1024]), W=[cprep])
        em.dma("pool", wsb[:], c_ws.rearrange("l g i j -> i (l g) j"), W=[cprep])
        with nc.allow_non_contiguous_dma(reason="8 scalars"):
            em.dma("sp", w00f[:], c_ws.rearrange("l g i j -> i (l g) j")[0:1, :, 0], W=[cprep])
        em.op("dve", lambda e: e.tensor_copy(out=w00[:], in_=w00f[:]), R=[cprep], W=[cprep])
        for lg in range(8):
            em.op("pe", lambda e: e.transpose(out=AT[:, lg * 128:(lg + 1) * 128], in_=wsb[:, lg, :], identity=ident),
                  R=[cprep, cbb], W=[ATb], sig=(lg == 7))
        em.op("dve", lambda e: e.tensor_tensor(
            out=trilT[:].rearrange("p l g i -> p (l g) i"), in0=AT[:].rearrange("p (a i) -> p a i", i=128),
            in1=mcur.unsqueeze(1).broadcast_to([128, 8, 128]), op=ALU.mult), R=[ATb, cbb], W=[cprep])
        if sample:
            pst = sb("pst", [15, 2, 1024])
            em.dma("sp", pst[:], pstate.rearrange("l r n -> r l n"), W=[pssb])
            for l in range(2):
                for c in range(8):
                    sl = rot("slot", NSL)
                    em.op("pe", lambda e: e.matmul(SBK[:, sl * 4:sl * 4 + 4], lhsT=pst[:, l, c * 128:(c + 1) * 128],
                                                   rhs=selt[:], start=True, stop=True), R=[pssb, smallb], W=[SBKb[sl]])
                    em.op("dve", lambda e: e.tensor_copy(out=pss[:, l, c, :], in_=SBK[:, sl * 4:sl * 4 + 4]),
                          R=[SBKb[sl]], W=[pssb])
            for l in range(2):
                with nc.allow_non_contiguous_dma(reason="dram row shift copy"):
                    em.dma("sp", pools[l, 0:14, :], pstate[l, 1:15, :])
                for g in range(3):
                    L = GRP[g][1]
                    em.dma("sp", kvs[g][l, 0:L - 1].rearrange("r a h d -> r (a h d)"),
                           cache[g][l, 1:L].rearrange("r a h d -> r (a h d)"))

        def wload(src, kcn, ncols):
            i = wst["i"]
            wst["i"] = (i + 1) % NWB
            n = kcn * ncols
            em.dma("pool", wb[i][:, 0:n], src, W=[wbb[i]])
            return i

        def wview(i, kcn, ncols):
            return wb[i][:, 0:kcn * ncols].rearrange("p (k n) -> p k n", k=kcn)

        def mm_group(lhs_fn, rhs_t, rhsb, kcn, has_s, Rw, rk=lambda kc: kc):
            b = rot("pj", 2)
            sl = rot("slot", NSL) if has_s else None
            for kc in range(kcn):
                last = kc == kcn - 1
                em.op("pe", lambda e: e.matmul(PJ[b][:, :], lhsT=lhs_fn(kc), rhs=rhs_t[:, rk(kc), 0:T],
                                               start=(kc == 0), stop=last),
                      R=Rw + rhsb, W=[PJb[b]], sig=(last and not has_s))
                if has_s:
                    em.op("pe", lambda e: e.matmul(SBK[:, sl * 4:sl * 4 + 1], lhsT=lhs_fn(kc),
                                                   rhs=rhs_t[:, rk(kc), SC:SC + 1], start=(kc == 0), stop=last),
                          R=Rw + rhsb, W=[SBKb[sl]], sig=last)
            parts = [(PJ[b][:, :], slice(0, T), [PJb[b]])]
            if has_s:
                parts.append((SBK[:, sl * 4:sl * 4 + 1], slice(SC, SC + 1), [SBKb[sl]]))
            return parts

        def proj(src, kcn, ncols, rhs_t, rhsb, has_s, handler, c0=0):
            i = wload(src, kcn, ncols)
            wv = wview(i, kcn, ncols)
            for ci in range(ncols // 128):
                parts = mm_group(lambda kc: wv[:, kc, ci * 128:(ci + 1) * 128], rhs_t, rhsb, kcn, has_s, [wbb[i]])
                for (p_ap, cs_, pb) in parts:
                    handler(c0 + ci, p_ap, cs_, pb)

        def wsrc(w2d, col0, ncols):
            return w2d.rearrange("(k p) n -> p k n", p=128)[:, :, col0:col0 + ncols]

        def rmsnorm(gvec, l, has_s, out_fn):
            TW = T + 1 if has_s else T
            b = rot("pj", 2)
            sl = rot("slot", NSL) if has_s else None
            for kc in range(KC):
                s = rot("sq", 2)
                em.op("act", lambda e: e.activation(out=sq[s][:, 0:TW], in_=xT[:, kc, 0:TW], func=AF.Square),
                      R=[xTb[kc]], W=[sqb[s]])
                last = kc == KC - 1
                em.op("pe", lambda e: e.matmul(PJ[b][:, :], lhsT=ones, rhs=sq[s][:, 0:T], start=(kc == 0), stop=last),
                      R=[sqb[s], cbb], W=[PJb[b]], sig=(last and not has_s) or True)
                if has_s:
                    em.op("pe", lambda e: e.matmul(SBK[:, sl * 4:sl * 4 + 1], lhsT=ones, rhs=sq[s][:, SC:SC + 1],
                                                   start=(kc == 0), stop=last), R=[sqb[s], cbb], W=[SBKb[sl]])
            em.op("act", lambda e: e.activation(out=rstd[:, 0:T], in_=PJ[b][:, :], func=AF.Sqrt, scale=1.0 / D, bias=EPS),
                  R=[PJb[b]], W=[rstdb])
            if has_s:
                em.op("act", lambda e: e.activation(out=rstd[:, SC:SC + 1], in_=SBK[:, sl * 4:sl * 4 + 1], func=AF.Sqrt,
                                                    scale=1.0 / D, bias=EPS), R=[SBKb[sl]], W=[rstdb])
            em.op("dve", lambda e: e.reciprocal(out=rstd[:, 0:TW], in_=rstd[:, 0:TW]), R=[rstdb], W=[rstdb])
            for kc in range(KC):
                out_fn(kc, TW)

        ptmp = sb("ptmp", [128, 16]); ptmpb = Buf("ptmp")
        WG = (128, 512, 2048)

        def kvout(l, g, kv, h, src_t, src_b, cs_, t0):
            if cs_.start == 0:
                lo = max(t0, SEQ - WG[g])
                if lo < t0 + T:
                    o = lo - (SEQ - WG[g])
                    em.dma("sp", kvp[g][l, kv, h][:, o:o + (t0 + T - lo)], src_t[:, lo - t0:T], R=[src_b])
            else:
                L = GRP[g][1]
                with nc.allow_non_contiguous_dma(reason="sample kv row"):
                    em.dma("sp", kvs[g][l, L - 1, kv, h, :].rearrange("(d o) -> d o", o=1), src_t[:, SC:SC + 1], R=[src_b])

        def attention(l, h, ti, has_s):
            t0 = ti * T
            pv = []
            for g in range(3):
                dil, halo, nb, nq = GRP[g]
                avail = min(halo, t0)
                KT = KTt[g]
                VT = VTt[g]
                kb = [KTh[g], KTc[g]]
                vb_ = [VTh[g], VTc[g]]
                cur, prev = [], []
                for b in range(nb):
                    if g == 0:
                        qc = slice(128 * b, 128 * b + 128)
                        cur.append((slice(halo + 128 * b, halo + 128 * b + 128), 128, b + 1, qc))
                        if b > 0 or avail:
                            prev.append((b, slice(halo + 128 * (b - 1), halo + 128 * b), 128, b, qc))
                    else:
                        qc = slice(b, T, dil)
                        cur.append((slice(halo + b, halo + T, dil), nq, b, qc))
                        if avail:
                            a = avail // dil if g == 2 else 128
                            prev.append((b, slice(halo - a * dil + b, halo, dil), a, nb + b, qc))
                vblocks = {}
                if g == 0:
                    if avail:
                        vblocks[0] = (slice(halo - 128, halo), 128)
                    for b in range(4):
                        vblocks[b + 1] = (slice(halo + 128 * b, halo + 128 * b + 128), 128)
                else:
                    for (ks, nk, vi, qc) in cur:
                        vblocks[vi] = (ks, nk)
                    for (b, ks, nk, vi, qc) in prev:
                        vblocks[vi] = (ks, nk)
                idxs = sorted(vblocks)
                for j0 in range(0, len(idxs), 8):
                    batch = idxs[j0:j0 + 8]
                    nkmax = max(vblocks[j][1] for j in batch)
                    for n_, j in enumerate(batch):
                        ks, nk = vblocks[j]
                        em.op("pe", lambda e: e.transpose(out=AT[0:nk, n_ * 128:(n_ + 1) * 128], in_=VT[:, ks], identity=ident),
                              R=vb_ + [cbb], W=[ATb], sig=(n_ == len(batch) - 1))
                    contiguous = batch == list(range(batch[0], batch[0] + len(batch)))
                    assert contiguous
                    em.op("act", lambda e: e.activation(
                        out=Vtok[g][0:nkmax, batch[0]:batch[0] + len(batch), :],
                        in_=AT[0:nkmax, 0:len(batch) * 128].rearrange("p (a d) -> p a d", d=128), func=AF.Copy),
                        R=[ATb], W=[Vtokb[g]])
                for kind in (0, 1):
                    blks = [(b, ks, nk, vi, qc) for b, (ks, nk, vi, qc) in enumerate(cur)] if kind == 0 else prev
                    if not blks:
                        continue
                    nk = blks[0][2]
                    for (b, ks, nk_, vi, qc) in blks:
                        em.op("pe", lambda e: e.matmul(AS[kind][0:nk, b * nq:(b + 1) * nq], lhsT=KT[:, ks], rhs=QT[:, g, qc],
                                                       start=True, stop=True),
                              R=kb + [QTb[g]], W=[ASb[kind]], sig=(b == blks[-1][0]))
                    c_lo = blks[0][0] * nq
                    c_hi = (blks[-1][0] + 1) * nq
                    nbk = len(blks)
                    em.op("act", lambda e: e.activation(out=PT[kind][0:nk, c_lo:c_hi], in_=AS[kind][0:nk, c_lo:c_hi],
                                                        func=AF.Exp, scale=QSCALE), R=[ASb[kind]], W=[PTb[kind]])
                    if kind == 0:
                        m = mcur[0:nk, 0:nq]
                    else:
                        m = mprev[0:nk, 0:nq] if nk == 128 else None
                    if m is not None:
                        em.op("dve", lambda e: e.tensor_tensor(
                            out=PT[kind][0:nk, c_lo:c_hi].rearrange("p (a q) -> p a q", q=nq),
                            in0=PT[kind][0:nk, c_lo:c_hi].rearrange("p (a q) -> p a q", q=nq),
                            in1=m.unsqueeze(1).broadcast_to([nk, nbk, nq]), op=ALU.mult),
                            R=[cbb], W=[PTb[kind]])
                    for (b, ks, nk_, vi, qc) in blks:
                        pv.append((g, vi, nk, kind, b * nq, nq, qc))
                for n_, (g_, vi, nk, kind, c0, nq_, qc) in enumerate(pv):
                    first = (g == 0 and n_ == 0)
                    em.op("pe", lambda e: e.matmul(AN[:, qc], lhsT=Vtok[g][0:nk, vi, :], rhs=PT[kind][0:nk, c0:c0 + nq_],
                                                   start=first, stop=False, skip_group_check=True),
                          R=[Vtokb[g], PTb[kind]], W=[ANb], sig=False)
                    em.op("pe", lambda e: e.matmul(AD[:, qc], lhsT=cbt[0:nk, 2, :], rhs=PT[kind][0:nk, c0:c0 + nq_],
                                                   start=first, stop=False, skip_group_check=True),
                          R=[PTb[kind], cbb], W=[ADb], sig=(n_ == len(pv) - 1))
                pv = []
            em.op("dve", lambda e: e.reciprocal(out=rD[:, 0:T], in_=AD[:, :]), R=[ADb], W=[rDb])
            em.op("dve", lambda e: e.tensor_tensor(out=oT[:, h, 0:T], in0=AN[:, :], in1=rD[:, 0:T], op=ALU.mult),
                  R=[ANb, rDb], W=[oTb[h]])
            if not has_s:
                return
            for g in range(3):
                dil, L, nb, nq = GRP[g]
                em.dma("pool", Kc[:, g, :], cache[g][l, 0:L:dil, 0, h, :], W=[Kcb])
                em.dma("pool", Vc[:, g, :], cache[g][l, 0:L:dil, 1, h, :], W=[Vcb])
            for g in range(3):
                halo = GRP[g][1]
                em.op("pe", lambda e: e.transpose(out=AT[:, g * 128:(g + 1) * 128], in_=Kc[:, g, :], identity=ident),
                      R=[Kcb, cbb], W=[ATb], sig=False)
                em.op("pe", lambda e: e.transpose(out=AT[0:1, (3 + g) * 128:(4 + g) * 128],
                                                  in_=VTt[g][:, halo + SC:halo + SC + 1], identity=ident),
                      R=[VTc[g], cbb], W=[ATb], sig=(g == 2))
            em.op("act", lambda e: e.activation(out=KcT[:].rearrange("p g d -> p (g d)"), in_=AT[:, 0:384], func=AF.Copy),
                  R=[ATb], W=[KcTb])
            em.op("act", lambda e: e.activation(out=vsr[:].rearrange("p g d -> p (g d)"), in_=AT[0:1, 384:768], func=AF.Copy),
                  R=[ATb], W=[vsrb])
            for g in range(3):
                halo = GRP[g][1]
                em.op("pe", lambda e: e.matmul(AS[0][:, g:g + 1], lhsT=KcT[:, g, :], rhs=QT[:, g, SC:SC + 1], start=True, stop=True),
                      R=[KcTb, QTb[g]], W=[ASb[0]], sig=False)
                em.op("pe", lambda e: e.matmul(AS[0][0:1, 4 + g:5 + g], lhsT=KTt[g][:, halo + SC:halo + SC + 1],
                                               rhs=QT[:, g, SC:SC + 1], start=True, stop=True),
                      R=[KTc[g], QTb[g]], W=[ASb[0]], sig=(g == 2))
            em.op("act", lambda e: e.activation(out=pS[:, 0:3], in_=AS[0][:, 0:3], func=AF.Exp, scale=QSCALE), R=[ASb[0]], W=[pSb])
            em.op("act", lambda e: e.activation(out=pS[0:1, 4:7], in_=AS[0][0:1, 4:7], func=AF.Exp, scale=QSCALE), R=[ASb[0]], W=[pSb])
            for which in (0, 1):
                for g in range(3):
                    lh1 = Vc[:, g, :] if which == 0 else ones
                    lh2 = vsr[0:1, g, :] if which == 0 else cbt[0:1, 2, :]
                    em.op("pe", lambda e: e.matmul(AS[1][:, which:which + 1], lhsT=lh1, rhs=pS[:, g:g + 1],
                                                   start=(g == 0), stop=False, skip_group_check=True),
                          R=[Vcb, pSb, cbb], W=[ASb[1]], sig=False)
                    em.op("pe", lambda e: e.matmul(AS[1][:, which:which + 1], lhsT=lh2, rhs=pS[0:1, 4 + g:5 + g],
                                                   start=False, stop=(g == 2), skip_group_check=True),
                          R=[vsrb, pSb, cbb], W=[ASb[1]], sig=(g == 2 and which == 1))
            em.op("dve", lambda e: e.reciprocal(out=rD[:, SC:SC + 1], in_=AS[1][:, 1:2]), R=[ASb[1]], W=[rDb])
            em.op("dve", lambda e: e.tensor_tensor(out=oT[:, h, SC:SC + 1], in0=AS[1][:, 0:1], in1=rD[:, SC:SC + 1], op=ALU.mult),
                  R=[ASb[1], rDb], W=[oTb[h]])

        def merge_branch(l, br, has_s):
            def gate_h(c, p_ap, cs_, pb):
                em.op("act", lambda e: e.activation(out=gateT[:, c, cs_], in_=p_ap, func=AF.Sigmoid,
                                                    bias=gbv[:, l, br * 16 + c:br * 16 + c + 1], scale=1.0),
                      R=pb + [smallb], W=[gateb[c]])
            for blk in range(4):
                proj(wsrc(w_in[l], O3 + br * 2048 + blk * 512, 512), KC, 512, hT, [hTb], has_s, gate_h, c0=blk * 4)

            def y_h(c, p_ap, cs_, pb):
                if br == 0:
                    em.op("dve", lambda e: e.tensor_tensor(out=mrg[:, c, cs_], in0=p_ap, in1=gateT[:, c, cs_], op=ALU.mult),
                          R=pb + [gateb[c]], W=[mrgb[c]])
                else:
                    s1 = rot("t1", 2)
                    em.op("dve", lambda e: e.tensor_tensor(out=t1[s1][:, cs_], in0=p_ap, in1=gateT[:, c, cs_], op=ALU.mult),
                          R=pb + [gateb[c]], W=[t1b[s1]])
                    em.op("dve", lambda e: e.tensor_tensor(out=mrg[:, c, cs_], in0=mrg[:, c, cs_], in1=t1[s1][:, cs_], op=ALU.add),
                          R=[t1b[s1]], W=[mrgb[c]])
            for half in range(2):
                proj(wsrc(wo[br][l], half * 1024, 1024), 8, 1024, oT, oTb, has_s, y_h, c0=half * 8)

        def mixer_b(l, ti, has_s, last_tile):
            barrier(G_B, G_C)
            if ti == 0:
                em.op("dve", lambda e: e.memset(xbT[:, :, 0:15], 0.0), W=xbb)
            else:
                em.op("dve", lambda e: e.tensor_copy(out=xbT[:, :, 0:15], in_=xbH[:, l, :, :]), R=[xbHb[l]], W=xbb)

            def xb_h(c, p_ap, cs_, pb):
                em.op("act", lambda e: e.activation(out=xbT[:, c, 15 + cs_.start:15 + cs_.stop], in_=p_ap, func=AF.Copy),
                      R=pb, W=[xbb[c]])
            for blk in range(2):
                proj(wsrc(w_in[l], O1 + blk * 512, 512), KC, 512, hT, [hTb], has_s, xb_h, c0=blk * 4)
            E_ = 15 + T
            for c in range(8):
                gi = c // 2
                w = WINS[gi]
                cur, curb, lo, sh, k = xbT[:, c, :], xbb[c], 0, 1, 0
                pp = [(pA, pAb), (pB, pBb)]
                while sh < w:
                    dst, dstb = pp[k % 2]
                    lo2 = lo + sh
                    em.op("dve", lambda e: e.tensor_tensor(out=dst[:, lo2:E_], in0=cur[:, lo2:E_], in1=cur[:, lo2 - sh:E_ - sh], op=ALU.add),
                          R=[curb], W=[dstb])
                    cur, curb, lo, sh, k = dst, dstb, lo2, sh * 2, k + 1
                em.op("dve", lambda e: e.scalar_tensor_tensor(out=dT[:, c, 0:T], in0=cur[:, 15:E_], scalar=1.0 / w, in1=xbT[:, c, 15:E_],
                                                              op0=ALU.mult, op1=ALU.subtract), R=[curb, xbb[c]], W=[dTb[c]])
                if ti == 0:
                    em.op("dve", lambda e: e.tensor_tensor(out=ptmp[:, 0:15], in0=cur[:, 15:30], in1=rct[:, gi, 0:15], op=ALU.mult),
                          R=[curb, smallb], W=[ptmpb])
                    em.op("dve", lambda e: e.tensor_tensor(out=dT[:, c, 0:15], in0=ptmp[:, 0:15], in1=xbT[:, c, 15:30], op=ALU.subtract),
                          R=[ptmpb, xbb[c]], W=[dTb[c]])
                if has_s:
                    xs_ = xbT[:, c, 15 + SC:15 + SC + 1]
                    em.op("dve", lambda e: e.tensor_tensor(out=ptmp[:, 15:16], in0=pss[:, l, c, gi:gi + 1], in1=xs_, op=ALU.add),
                          R=[pssb, xbb[c]], W=[ptmpb])
                    em.op("dve", lambda e: e.scalar_tensor_tensor(out=dT[:, c, SC:SC + 1], in0=ptmp[:, 15:16], scalar=1.0 / w, in1=xs_,
                                                                  op0=ALU.mult, op1=ALU.subtract), R=[ptmpb, xbb[c]], W=[dTb[c]])
            em.op("dve", lambda e: e.tensor_copy(out=xbH[:, l, :, :], in_=xbT[:, :, T:T + 15]), R=xbb, W=[xbHb[l]])
            if last_tile:
                em.dma("sp", poolp[l].rearrange("(c p) r -> p c r", p=128), xbT[:, :, T:T + 15], R=xbb)
            if has_s:
                with nc.allow_non_contiguous_dma(reason="sample pool row"):
                    em.dma("sp", pools[l, 14].rearrange("(c p) -> p c", p=128), xbT[:, :, 15 + SC], R=xbb)
            i = wload(pool_w[l].rearrange("g (ki p) m -> p (g ki) m", p=128), 8, 256)
            wv = wview(i, 8, 256)
            for gi in range(4):
                for mo in range(2):
                    c = gi * 2 + mo
                    parts = mm_group(lambda kc: wv[:, gi * 2 + kc, mo * 128:(mo + 1) * 128], dT, dTb, 2, has_s, [wbb[i]],
                                     rk=lambda kc: gi * 2 + kc)
                    for (p_ap, cs_, pb) in parts:
                        em.op("act", lambda e: e.activation(out=oT[:, c, cs_], in_=p_ap, func=AF.Copy, scale=psv[:, l, c:c + 1]),
                              R=pb + [smallb], W=[oTb[c]])

        def mixer_c(l, ti, has_s):
            barrier(G_C, G_B)
            def u_h(c, p_ap, cs_, pb):
                em.op("act", lambda e: e.activation(out=uT[:, c, cs_], in_=p_ap, func=AF.Gelu_apprx_tanh), R=pb, W=[uTb[c]])
            for blk in range(2):
                proj(wsrc(w_in[l], O2 + blk * 512, 512), KC, 512, hT, [hTb], has_s, u_h, c0=blk * 4)
            chk('c_u')
            em.dma("sp", gnr[:], c_norm_g[l:l + 1, :].broadcast_to([128, 1024]), W=[gnrb])
            iw = [wload(wsrc(w_in[l], O2 + 1024 + hf * 512, 512), KC, 512) for hf in range(2)]
            wvs = [wview(i, KC, 512) for i in iw]
            for tb in range(5 if has_s else 4):
                M = 128 if tb < 4 else 1
                c_lo = tb * 128 if tb < 4 else SC
                for hf in range(2):
                    b = rot("pj", 2)
                    for kc in range(KC):
                        em.op("pe", lambda e: e.matmul(PJ[b][0:M, :], lhsT=hT[:, kc, c_lo:c_lo + M], rhs=wvs[hf][:, kc, :],
                                                       start=(kc == 0), stop=(kc == KC - 1)),
                              R=[hTb, wbb[iw[hf]]], W=[PJb[b]], sig=(kc == KC - 1))
                    em.op("act", lambda e: e.activation(out=gv[0:M, hf * 512:(hf + 1) * 512], in_=PJ[b][0:M, :], func=AF.Gelu_apprx_tanh),
                          R=[PJb[b]], W=[gvb])
                em.op("dve", lambda e: e.scalar_tensor_tensor(out=gsq[0:M, :], in0=gv[0:M, :], scalar=1.0, in1=gv[0:M, :],
                                                              op0=ALU.mult, op1=ALU.mult, accum_out=ssq[0:M, 0:1]),
                      R=[gvb], W=[gsqb, ssqb])
                em.op("act", lambda e: e.activation(out=ssq[0:M, 1:2], in_=ssq[0:M, 0:1], func=AF.Sqrt, scale=1.0 / 1024, bias=EPS),
                      R=[ssqb], W=[ssqb])
                em.op("dve", lambda e: e.reciprocal(out=ssq[0:M, 2:3], in_=ssq[0:M, 1:2]), R=[ssqb], W=[ssqb])
                if tb < 4:
                    em.op("dve", lambda e: e.scalar_tensor_tensor(out=vcb[:, tb, :], in0=gv[:, :], scalar=ssq[:, 2:3], in1=gnr[:, :],
                                                                  op0=ALU.mult, op1=ALU.mult), R=[gvb, ssqb, gnrb], W=[vcbb[tb]])
                else:
                    em.op("dve", lambda e: e.scalar_tensor_tensor(out=vcs[0:1, :], in0=gv[0:1, :], scalar=ssq[0:1, 2:3], in1=gnr[0:1, :],
                                                                  op0=ALU.mult, op1=ALU.mult), R=[gvb, ssqb, gnrb], W=[vcsb])
                    em.op("dve", lambda e: e.tensor_copy(out=vcb[0:1, 4, :], in_=vcs[0:1, :]), R=[vcsb], W=[vcbb[4]])
                    em.dma("sp", cvs[l:l + 1, :], vcs[0:1, :], R=[vcsb])
            chk('c_v')
            for cc in range(8):
                g = cc // 2
                a = cc % 2
                for tb in range(4):
                    em.op("pe", lambda e: e.matmul(AS[a][:, tb * 128:(tb + 1) * 128], lhsT=vcb[:, tb, cc * 128:(cc + 1) * 128],
                                                   rhs=trilT[:, l, g, :], start=True, stop=True),
                          R=[vcbb[tb], cprep], W=[ASb[a]], sig=(tb == 3))
                s1 = rot("t1", 2)
                em.op("dve", lambda e: e.tensor_tensor(
                    out=t1[s1][:, 0:T].rearrange("p (a i) -> p a i", i=128), in0=AS[a][:, :].rearrange("p (a i) -> p a i", i=128),
                    in1=biasr[:, l, g, :].unsqueeze(1).broadcast_to([128, 4, 128]), op=ALU.add), R=[ASb[a], cprep], W=[t1b[s1]])
                em.op("dve", lambda e: e.tensor_tensor(out=oT[:, cc, 0:T], in0=t1[s1][:, 0:T], in1=uT[:, cc, 0:T], op=ALU.mult),
                      R=[t1b[s1], uTb[cc]], W=[oTb[cc]])
                if has_s:
                    sl = rot("slot", NSL)
                    em.op("pe", lambda e: e.matmul(SBK[:, sl * 4:sl * 4 + 1], lhsT=vcb[0:1, 4, cc * 128:(cc + 1) * 128],
                                                   rhs=w00[0:1, l * 4 + g:l * 4 + g + 1], start=True, stop=True),
                          R=[vcbb[4], cprep], W=[SBKb[sl]])
                    em.op("dve", lambda e: e.scalar_tensor_tensor(out=oT[:, cc, SC:SC + 1], in0=SBK[:, sl * 4:sl * 4 + 1],
                                                                  scalar=biasr[:, l, g, 0:1], in1=uT[:, cc, SC:SC + 1],
                                                                  op0=ALU.add, op1=ALU.mult), R=[SBKb[sl], cprep, uTb[cc]], W=[oTb[cc]])

        try:
            chk('init')
            for ti in range(NT):
                em.epoch = ti
                has_s = sample and ti == 0
                TW = T + 1 if has_s else T
                t0 = ti * T
                last_tile = ti == NTILES - 1
                for c4 in range(4):
                    em.dma("sp", xT[:, c4 * 4:(c4 + 1) * 4, 0:T],
                           xT_in.rearrange("(c p) t -> p c t", p=128)[:, c4 * 4:(c4 + 1) * 4, t0:t0 + T],
                           W=xTb[c4 * 4:(c4 + 1) * 4])
                if has_s:
                    with nc.allow_non_contiguous_dma(reason="sample column"):
                        em.dma("sp", xT[:, :, SC:SC + 1], xs_in.rearrange("(c p) o -> p c o", p=128), W=xTb)
                em.dma("sp", cst[:], cs_in[ti].rearrange("a p n -> p a n"), W=[cstb])
                cosT = cst[:, 0, :]
                sinT = cst[:, 1, :]

                for l in range(NL):
                    w_in_l = w_in[l]
                    def n1(kc, TW_):
                        em.op("dve", lambda e: e.scalar_tensor_tensor(
                            out=hT[:, kc, 0:TW_], in0=xT[:, kc, 0:TW_], scalar=g1v[:, l, kc:kc + 1], in1=rstd[:, 0:TW_],
                            op0=ALU.mult, op1=ALU.mult), R=[xTb[kc], rstdb, smallb], W=[hTb])
                    rmsnorm(g1v, l, has_s, n1)
                    chk('norm1')

                    barrier(G_ATT + oTb, G_FFN + G_MRG + [cprep])
                    for h in range(8):
                        for g in range(3):
                            dil, halo, nb, nq = GRP[g]
                            avail = min(halo, t0)
                            if avail:
                                em.dma("sp", KTt[g][:, halo - avail:halo], hist[l, g, 0, h][:, t0 - avail:t0], W=[KTh[g]])
                                em.dma("sp", VTt[g][:, halo - avail:halo], hist[l, g, 1, h][:, t0 - avail:t0], W=[VTh[g]])
                            base = g * 3072 + h * 128
                            i = wst["i"]
                            wst["i"] = (i + 1) % NWB
                            wv = wb[i][:, 0:KC * 384].rearrange("p (k t n) -> p k t n", k=KC, t=3)
                            em.drain("pool", [wbb[i]])
                            tk3 = []
                            for tq in range(3):
                                src = w_in_l.rearrange("(k p) n -> p k n", p=128)[:, :, base + tq * 1024:base + tq * 1024 + 128]
                                tk3.append(em.dma("pool", wv[:, :, tq, :], src))
                            wbb[i].w = tk3
                            wbb[i].r = []
                            for tq in range(3):
                                parts = mm_group(lambda kc: wv[:, kc, tq, :], hT, [hTb], KC, has_s, [wbb[i]])
                                for (p_ap, cs_, pb) in parts:
                                    n_ = cs_.stop - cs_.start
                                    if tq == 2:
                                        em.op("act", lambda e: e.activation(
                                            out=VTt[g][:, halo + cs_.start:halo + cs_.stop], in_=p_ap, func=AF.Copy),
                                            R=pb, W=[VTc[g]])
                                        s1 = rot("t1", 2)
                                        em.op("dve", lambda e: e.tensor_copy(out=t1[s1][:, cs_], in_=p_ap), R=pb, W=[t1b[s1]])
                                        kvout(l, g, 1, h, t1[s1], t1b[s1], cs_, t0)
                                        continue
                                    em.op("act", lambda e: e.activation(out=zb[:, cs_], in_=p_ap, func=AF.Copy), R=pb, W=[zbb])
                                    a = 0 if cs_.start == 0 else 1
                                    if a == 0:
                                        em.op("pe", lambda e: e.matmul(AS[0][:, :], lhsT=rswap, rhs=zb[:, cs_], start=True, stop=True),
                                              R=[zbb, cbb], W=[ASb[0]])
                                        r_ap = AS[0][:, :]; rb = [ASb[0]]
                                    else:
                                        em.op("pe", lambda e: e.matmul(AS[1][:, 0:1], lhsT=rswap, rhs=zb[:, cs_], start=True, stop=True),
                                              R=[zbb, cbb], W=[ASb[1]])
                                        r_ap = AS[1][:, 0:1]; rb = [ASb[1]]
                                    s1 = rot("t1", 2)
                                    em.op("dve", lambda e: e.tensor_tensor(out=t1[s1][:, cs_], in0=p_ap, in1=cosT[:, cs_], op=ALU.mult),
                                          R=pb + [cstb], W=[t1b[s1]])
                                    em.op("dve", lambda e: e.tensor_tensor(out=t2[:, cs_], in0=r_ap, in1=sinT[:, cs_], op=ALU.mult),
                                          R=rb + [cstb], W=[t2b])
                                    if tq == 0:
                                        em.op("dve", lambda e: e.tensor_tensor(out=QT[:, g, cs_], in0=t1[s1][:, cs_], in1=t2[:, cs_], op=ALU.add),
                                              R=[t1b[s1], t2b], W=[QTb[g]])
                                    else:
                                        em.op("dve", lambda e: e.tensor_tensor(out=t1[s1][:, cs_], in0=t1[s1][:, cs_], in1=t2[:, cs_], op=ALU.add),
                                              R=[t2b], W=[t1b[s1]])
                                        em.op("act", lambda e: e.activation(
                                            out=KTt[g][:, halo + cs_.start:halo + cs_.stop], in_=t1[s1][:, cs_], func=AF.Copy),
                                            R=[t1b[s1]], W=[KTc[g]])
                                        kvout(l, g, 0, h, t1[s1], t1b[s1], cs_, t0)
                            if ti < NT - 1:
                                em.dma("sp", hist[l, g, 0, h][:, t0:t0 + T], KTt[g][:, halo:halo + T], R=[KTc[g]])
                                em.dma("sp", hist[l, g, 1, h][:, t0:t0 + T], VTt[g][:, halo:halo + T], R=[VTc[g]])
                        chk('qkv')
                        attention(l, h, ti, has_s)
                        chk('attn')

                    barrier(G_MRG, G_ATT)
                    merge_branch(l, 0, has_s)
                    chk('mrgA')
                    mixer_b(l, ti, has_s, last_tile)
                    chk('mixB')
                    merge_branch(l, 1, has_s)
                    mixer_c(l, ti, has_s)
                    chk('mixC')
                    merge_branch(l, 2, has_s)
                    chk('mrgC')
                    def resid(c, p_ap, cs_, pb):
                        em.op("dve", lambda e: e.tensor_tensor(out=xT[:, c, cs_], in0=xT[:, c, cs_], in1=p_ap, op=ALU.add),
                              R=pb, W=[xTb[c]])
                    for blk in range(4):
                        proj(wsrc(w_out[l], blk * 512, 512), KC, 512, mrg, mrgb, has_s, resid, c0=blk * 4)
                    barrier(G_FFN, G_MRG + G_ATT + oTb)
                    def n2(kc, TW_):
                        em.op("dve", lambda e: e.scalar_tensor_tensor(
                            out=hT[:, kc, 0:TW_], in0=xT[:, kc, 0:TW_], scalar=g2v[:, l, kc:kc + 1], in1=rstd[:, 0:TW_],
                            op0=ALU.mult, op1=ALU.mult), R=[xTb[kc], rstdb, smallb], W=[hTb])
                    chk('wout')
                    rmsnorm(g2v, l, has_s, n2)
                    chk('norm2')
                    for blk in range(FFH // 512):
                        ig = wload(wsrc(w_gu[l], blk * 512, 512), KC, 512)
                        iu = wload(wsrc(w_gu[l], FFH + blk * 512, 512), KC, 512)
                        wg_ = wview(ig, KC, 512)
                        wu_ = wview(iu, KC, 512)
                        for ci in range(4):
                            i_ = blk * 4 + ci
                            pg = mm_group(lambda kc: wg_[:, kc, ci * 128:(ci + 1) * 128], hT, [hTb], KC, has_s, [wbb[ig]])
                            s_ = rot("sg", 2)
                            for (p_ap, cs_, pb) in pg:
                                em.op("act", lambda e: e.activation(out=sg[s_][:, cs_], in_=p_ap, func=AF.Silu), R=pb, W=[sgb[s_]])
                            pu = mm_group(lambda kc: wu_[:, kc, ci * 128:(ci + 1) * 128], hT, [hTb], KC, has_s, [wbb[iu]])
                            for (p_ap, cs_, pb) in pu:
                                em.op("dve", lambda e: e.tensor_tensor(out=actT[:, i_, cs_], in0=p_ap, in1=sg[s_][:, cs_], op=ALU.mult),
                                      R=pb + [sgb[s_]], W=[actb[i_]])
                            chk('ffn_c%d' % i_)
                    chk('ffn')
                    for c in range(KC):
                        src = w_down[l].rearrange("(k p) n -> p k n", p=128)[:, :, c * 128:(c + 1) * 128]
                        proj(src, 44, 128, actT, actb, has_s, resid, c0=c)

                chk('wdown')
                def nf(kc, TW_):
                    s_ = rot("yt", 2)
                    em.op("dve", lambda e: e.scalar_tensor_tensor(
                        out=yt[s_][:, 0:TW_], in0=xT[:, kc, 0:TW_], scalar=gfv[:, 0, kc:kc + 1], in1=rstd[:, 0:TW_],
                        op0=ALU.mult, op1=ALU.mult), R=[xTb[kc], rstdb, smallb], W=[ytb[s_]])
                    em.dma("sp", yT[kc * 128:(kc + 1) * 128, t0:t0 + T], yt[s_][:, 0:T], R=[ytb[s_]])
                    if TW_ > T:
                        with nc.allow_non_contiguous_dma(reason="sample column"):
                            em.dma("sp", ys[kc * 128:(kc + 1) * 128, :], yt[s_][:, SC:SC + 1], R=[ytb[s_]])
                rmsnorm(gfv, 0, has_s, nf)
        except _Stop:
            pass
        print("emitted ops", em.n_op, "instructions", em.n_ins, flush=True)
        fin = [(k, v) for k, v in em.cnt.items() if v]
        em._wait("sp", fin)
    return nc


OUT_NAMES = ("yT", "ys", "kvp1", "kvp2", "kvp3", "kvs1", "kvs2", "kvs3", "poolp", "pools", "cvs")


def make_in_maps(inp, cores):
    cs, cb, rc, sel = _consts()
    f = lambda a: np.ascontiguousarray(np.asarray(a, dtype=np.float32))
    xTs = {}
    shared = dict(
        norm1_g=f(inp["norm1_g"]), w_in=f(inp["w_in"]), wo_a=f(inp["wo_a"]), wo_b=f(inp["wo_b"]), wo_c=f(inp["wo_c"]),
        pool_w=f(inp["pool_w"]), pool_scale=f(inp["pool_scale"]), c_norm_g=f(inp["c_norm_g"]), c_ws=f(inp["c_ws"]),
        c_bias=f(inp["c_bias"]), gate_bias=f(inp["gate_bias"]), w_out=f(inp["w_out"]), norm2_g=f(inp["norm2_g"]),
        w_gu=f(inp["w_gu"]), w_down=f(inp["w_down"]), final_g=f(inp["final_norm_g"])[None, :],
        cs_in=cs, cb_in=cb, rc_in=rc, sel_in=sel)
    maps = []
    for c in cores:
        b = c % 2
        if b not in xTs:
            xTs[b] = f(np.asarray(inp["x_prompt"])[b].T)
        m = dict(shared)
        m.update(xT_in=xTs[b], xs_in=f(np.asarray(inp["x_sample"])[c, 0][:, None]),
                 c1=f(np.asarray(inp["cache_a1_kv"])[:, c]), c2=f(np.asarray(inp["cache_a2_kv"])[:, c]),
                 c3=f(np.asarray(inp["cache_a3_kv"])[:, c]), pstate=f(np.asarray(inp["state_b_pool"])[:, c]))
        maps.append(m)
    return maps


def assemble(results):
    r = results
    y_prompt = np.stack([r[b]["yT"].T for b in range(2)])
    y_sample = np.stack([r[c]["ys"][:, 0][None, :] for c in range(8)])
    outs = [y_prompt, y_sample]
    for g in range(3):
        outs.append(np.stack([np.transpose(r[b][f"kvp{g + 1}"], (0, 4, 1, 2, 3)) for b in range(2)], axis=1))
    for g in range(3):
        outs.append(np.stack([r[c][f"kvs{g + 1}"] for c in range(8)], axis=1))
    outs.append(np.stack([np.transpose(r[b]["poolp"], (0, 2, 1)) for b in range(2)], axis=1))
    outs.append(np.stack([r[c]["pools"] for c in range(8)], axis=1))
    outs.append(np.stack([r[c]["cvs"][:, None, :] for c in range(8)], axis=1))
    return tuple(np.ascontiguousarray(o, dtype=np.float32) for o in outs)


def kernel(**inp):
    nc = build()
    maps = make_in_maps(inp, list(range(8)))
    res = run_bass_kernel_spmd(nc, maps, core_ids=list(range(8)))
    return assemble(res.results)
```
